# Optimizing a Trainium2 kernel written in Bass

```python
import math
import jax, jax.numpy as jnp
from jax import lax
import numpy as np

D_MODEL = 1024
BATCH = 16
SEQ = 2048
DEPTH = 4
DEC_BATCH = 2
DEC_SEQ = 16384
PAST_LEN = 128

HEAD_DIM = 64
ATTN_WIDTH = 3 * D_MODEL // 8
ATTN_HEADS = ATTN_WIDTH // HEAD_DIM
ATTN_KV_HEADS = ATTN_HEADS // 3
KV_WIDTH = ATTN_KV_HEADS * HEAD_DIM
ATTN_WINDOW = 128
ATTN_BLOCK = 128
ROPE_THETA = 500000.0
ROPE_DIM = HEAD_DIM // 4
SSM_WIDTH = D_MODEL // 4
SSM_GROUP = 16
SSM_GROUPS = SSM_WIDTH // SSM_GROUP
SSM_STATE = 64
RET_WIDTH = 3 * D_MODEL // 8
RET_HEADS = RET_WIDTH // HEAD_DIM
RET_CHUNK = 128
RET_THETA = 10000.0
MIX_WIDTH = ATTN_WIDTH + SSM_WIDTH + RET_WIDTH
IN_SPLITS = (ATTN_WIDTH, KV_WIDTH, KV_WIDTH, SSM_WIDTH, RET_WIDTH, RET_WIDTH, RET_WIDTH, RET_WIDTH)
IN_WIDTH = sum(IN_SPLITS)
D_FF = 2816
FFN_CONV = 3
DN_ALPHA = (2 * DEPTH) ** 0.25
DN_BETA = (8 * DEPTH) ** -0.25
EPS = 1e-5

kernel_name = "hymba_style_bidir_encoder"

F32 = jnp.float32


def layer_norm(x, g, b):
    xf = x.astype(F32)
    mu = jnp.mean(xf, -1, keepdims=True)
    xc = xf - mu
    var = jnp.mean(xc * xc, -1, keepdims=True)
    return (xc * lax.rsqrt(var + EPS) * g.astype(F32) + b.astype(F32)).astype(x.dtype)


def rms_norm(x, g):
    xf = x.astype(F32)
    return xf * lax.rsqrt(jnp.mean(xf * xf, -1, keepdims=True) + EPS) * g.astype(F32)


def rotate(x, inv_freq):
    r = 2 * inv_freq.shape[0]
    L = x.shape[1]
    ang = jnp.arange(L, dtype=F32)[:, None] * inv_freq[None, :]
    cos = jnp.cos(ang)[None, :, None, :].astype(x.dtype)
    sin = jnp.sin(ang)[None, :, None, :].astype(x.dtype)
    x1 = x[..., : r // 2]
    x2 = x[..., r // 2: r]
    return jnp.concatenate([x1 * cos - x2 * sin, x1 * sin + x2 * cos, x[..., r:]], axis=-1)


def windowed_attention(q, k, v, sink):
    Bsz, L, H, hd = q.shape
    KV = k.shape[2]
    G = H // KV
    BLK = ATTN_BLOCK
    nb = L // BLK
    qb = q.reshape(Bsz, nb, BLK, KV, G, hd)
    pad = ((0, 0), (BLK, BLK), (0, 0), (0, 0))
    kp = jnp.pad(k, pad).reshape(Bsz, nb + 2, BLK, KV, hd)
    vp = jnp.pad(v, pad).reshape(Bsz, nb + 2, BLK, KV, hd)
    kw = jnp.concatenate([kp[:, :-2], kp[:, 1:-1], kp[:, 2:]], axis=2)
    vw = jnp.concatenate([vp[:, :-2], vp[:, 1:-1], vp[:, 2:]], axis=2)
    s = jnp.einsum('bnqkgd,bnskd->bnkgqs', qb, kw, preferred_element_type=F32) * (hd ** -0.5)
    qi = jnp.arange(BLK)
    kj = jnp.arange(3 * BLK) - BLK
    band = jnp.abs(kj[None, :] - qi[:, None]) <= ATTN_WINDOW
    kabs = jnp.arange(nb)[:, None] * BLK + kj[None, :]
    inside = (kabs >= 0) & (kabs < L)
    mask = band[None, :, :] & inside[:, None, :]
    s = jnp.where(mask[None, :, None, None], s, -1e30)
    sk = sink.astype(F32).reshape(KV, G)[None, None, :, :, None, None]
    m = jnp.maximum(jnp.max(s, -1, keepdims=True), sk)
    p = jnp.exp(s - m)
    p = p / (jnp.sum(p, -1, keepdims=True) + jnp.exp(sk - m))
    o = jnp.einsum('bnkgqs,bnskd->bnqkgd', p.astype(v.dtype), vw)
    return o.reshape(Bsz, L, H, hd)


def _linear_combine(left, right):
    a1, b1 = left
    a2, b2 = right
    return a1 * a2, a2 * b1 + b2


def s5_bidirectional(u, lam_re, lam_im, log_dt, b_re, b_im, c_re, c_im, d_skip):
    Bsz, L, _ = u.shape
    uf = u.astype(F32).reshape(Bsz, L, SSM_GROUPS, SSM_GROUP)
    uc = uf.astype(jnp.complex64)
    y = d_skip.astype(F32).reshape(SSM_GROUPS, SSM_GROUP) * uf
    for z in range(2):
        lam = lax.complex(lam_re[z].astype(F32), lam_im[z].astype(F32))
        dt = jnp.exp(log_dt[z].astype(F32))[:, None]
        lam_bar = jnp.exp(lam * dt)
        b_bar = ((lam_bar - 1.0) / lam)[:, :, None] * lax.complex(b_re[z].astype(F32), b_im[z].astype(F32))
        bu = jnp.einsum('blgh,gph->blgp', uc, b_bar)
        a = jnp.broadcast_to(lam_bar, bu.shape)
        _, states = lax.associative_scan(_linear_combine, (a, bu), axis=1, reverse=(z == 1))
        c = lax.complex(c_re[z].astype(F32), c_im[z].astype(F32))
        y = y + jnp.real(jnp.einsum('blgp,ghp->blgh', states, c))
    return y.reshape(Bsz, L, SSM_WIDTH)


def retention(q, k, v):
    Bsz, L, H, d = q.shape
    C = RET_CHUNK
    nc = L // C
    log_g = jnp.log(1.0 - 2.0 ** (-5.0 - jnp.arange(H, dtype=F32)))
    idx = jnp.arange(C, dtype=F32)
    inv_freq = 1.0 / (RET_THETA ** jnp.linspace(0.0, 1.0, d // 2, dtype=F32))
    q = rotate(q.astype(F32), inv_freq)
    k = rotate(k.astype(F32), inv_freq) * (d ** -0.5)
    qc = q.reshape(Bsz, nc, C, H, d)
    kc = k.reshape(Bsz, nc, C, H, d)
    vc = v.astype(F32).reshape(Bsz, nc, C, H, d)
    decay = jnp.exp(log_g[:, None, None] * jnp.abs(idx[:, None] - idx[None, :]))
    s = jnp.einsum('bnihd,bnjhd->bnhij', qc, kc) * decay
    out = jnp.einsum('bnhij,bnjhe->bnihe', s, vc)
    k_fwd = kc * jnp.exp(log_g[None, :] * (C - 1.0 - idx)[:, None])[:, :, None]
    k_bwd = kc * jnp.exp(log_g[None, :] * idx[:, None])[:, :, None]
    r_fwd = jnp.einsum('bnjhd,bnjhe->nbhde', k_fwd, vc)
    r_bwd = jnp.einsum('bnjhd,bnjhe->nbhde', k_bwd, vc)
    g_chunk = jnp.exp(log_g * C)[None, :, None, None]

    def step(state, r):
        return g_chunk * state + r, state

    init = jnp.zeros((Bsz, H, d, d), F32)
    _, s_fwd = lax.scan(step, init, r_fwd)
    _, s_bwd = lax.scan(step, init, r_bwd, reverse=True)
    q_fwd = qc * jnp.exp(log_g[None, :] * (idx + 1.0)[:, None])[:, :, None]
    q_bwd = qc * jnp.exp(log_g[None, :] * (C - idx)[:, None])[:, :, None]
    out = out + jnp.einsum('bnihd,nbhde->bnihe', q_fwd, s_fwd) + jnp.einsum('bnihd,nbhde->bnihe', q_bwd, s_bwd)
    return out.reshape(Bsz, L, H, d)


def conv_ffn(x, w_up, conv_w, conv_b, w_down):
    h = x @ w_up
    val, act = jnp.split(h, 2, axis=-1)
    act = lax.conv_general_dilated(act, conv_w[:, None, :], window_strides=(1,), padding=((1, 1),),
                                   dimension_numbers=('NWC', 'WIO', 'NWC'), feature_group_count=D_FF) + conv_b
    return (jax.nn.silu(act) * val) @ w_down


def encoder_layer(x, w_in, attn_sink, attn_out_g, lam_re, lam_im, log_dt, b_re, b_im, c_re, c_im,
                  ssm_d, glu_w, glu_b, ssm_out_g, w_out, ln1_g, ln1_b, w_up, conv_w, conv_b, w_down,
                  ln2_g, ln2_b):
    Bsz, L, _ = x.shape
    h = x @ w_in
    splits = np.cumsum(IN_SPLITS)[:-1].tolist()
    q, k, v, u, rq, rk, rv, rg = jnp.split(h, splits, axis=-1)
    inv = ROPE_THETA ** (-jnp.arange(0, ROPE_DIM, 2, dtype=F32) / ROPE_DIM)
    q = rotate(q.reshape(Bsz, L, ATTN_HEADS, HEAD_DIM), inv)
    k = rotate(k.reshape(Bsz, L, ATTN_KV_HEADS, HEAD_DIM), inv)
    v = v.reshape(Bsz, L, ATTN_KV_HEADS, HEAD_DIM)
    a = windowed_attention(q, k, v, attn_sink).reshape(Bsz, L, ATTN_WIDTH)
    a = rms_norm(a, attn_out_g).astype(x.dtype)
    s = jax.nn.gelu(s5_bidirectional(u, lam_re, lam_im, log_dt, b_re, b_im, c_re, c_im, ssm_d))
    s = s * jax.nn.sigmoid(s @ glu_w.astype(F32) + glu_b.astype(F32))
    s = rms_norm(s, ssm_out_g).astype(x.dtype)
    r = retention(rq.reshape(Bsz, L, RET_HEADS, HEAD_DIM), rk.reshape(Bsz, L, RET_HEADS, HEAD_DIM),
                  rv.reshape(Bsz, L, RET_HEADS, HEAD_DIM))
    mu = jnp.mean(r, -1, keepdims=True)
    rc = r - mu
    r = rc * lax.rsqrt(jnp.mean(rc * rc, -1, keepdims=True) + EPS)
    r = (r.reshape(Bsz, L, RET_WIDTH) * jax.nn.silu(rg.astype(F32))).astype(x.dtype)
    mix = jnp.concatenate([a, s, r], axis=-1) @ w_out
    x = layer_norm(DN_ALPHA * x + mix, ln1_g, ln1_b)
    x = layer_norm(DN_ALPHA * x + conv_ffn(x, w_up, conv_w, conv_b, w_down), ln2_g, ln2_b)
    return x


def trunk(x, params):
    for l in range(DEPTH):
        x = encoder_layer(x, *[p[l] for p in params])
    return x


def setup_inputs(seed: int = 0) -> dict:
    key = jax.random.key(seed)
    ks = jax.random.split(key, 26)
    nrm = lambda k, shape, scale: jax.random.normal(k, shape, F32) * scale
    G, P, Hg = SSM_GROUPS, SSM_STATE, SSM_GROUP
    lam_im_base = jnp.pi * jnp.arange(P, dtype=F32)
    return {
        "x_prompt": nrm(ks[0], (BATCH, SEQ, D_MODEL), 1.0),
        "x_sample": nrm(ks[1], (DEC_BATCH, DEC_SEQ, D_MODEL), 1.0),
        "w_in": nrm(ks[2], (DEPTH, D_MODEL, IN_WIDTH), D_MODEL ** -0.5),
        "attn_sink": nrm(ks[3], (DEPTH, ATTN_HEADS), 0.5),
        "attn_out_g": 1.0 + nrm(ks[4], (DEPTH, ATTN_WIDTH), 0.02),
        "ssm_lambda_re": -0.5 + nrm(ks[5], (DEPTH, 2, G, P), 0.01),
        "ssm_lambda_im": lam_im_base + nrm(ks[6], (DEPTH, 2, G, P), 0.01),
        "ssm_log_dt": jax.random.uniform(ks[7], (DEPTH, 2, G), F32, math.log(0.001), math.log(0.1)),
        "ssm_b_re": nrm(ks[8], (DEPTH, 2, G, P, Hg), (2 * Hg) ** -0.5),
        "ssm_b_im": nrm(ks[9], (DEPTH, 2, G, P, Hg), (2 * Hg) ** -0.5),
        "ssm_c_re": nrm(ks[10], (DEPTH, 2, G, Hg, P), (2 * P) ** -0.5),
        "ssm_c_im": nrm(ks[11], (DEPTH, 2, G, Hg, P), (2 * P) ** -0.5),
        "ssm_d": nrm(ks[12], (DEPTH, SSM_WIDTH), 1.0),
        "ssm_glu_w": nrm(ks[13], (DEPTH, SSM_WIDTH, SSM_WIDTH), SSM_WIDTH ** -0.5),
        "ssm_glu_b": nrm(ks[14], (DEPTH, SSM_WIDTH), 0.01),
        "ssm_out_g": 1.0 + nrm(ks[15], (DEPTH, SSM_WIDTH), 0.02),
        "w_out": nrm(ks[16], (DEPTH, MIX_WIDTH, D_MODEL), DN_BETA * MIX_WIDTH ** -0.5),
        "ln1_g": 1.0 + nrm(ks[17], (DEPTH, D_MODEL), 0.02),
        "ln1_b": nrm(ks[18], (DEPTH, D_MODEL), 0.02),
        "ffn_w_up": nrm(ks[19], (DEPTH, D_MODEL, 2 * D_FF), D_MODEL ** -0.5),
        "ffn_conv_w": nrm(ks[20], (DEPTH, FFN_CONV, D_FF), FFN_CONV ** -0.5),
        "ffn_conv_b": nrm(ks[21], (DEPTH, D_FF), 0.01),
        "ffn_w_down": nrm(ks[22], (DEPTH, D_FF, D_MODEL), DN_BETA * D_FF ** -0.5),
        "ln2_g": 1.0 + nrm(ks[23], (DEPTH, D_MODEL), 0.02),
        "ln2_b": nrm(ks[24], (DEPTH, D_MODEL), 0.02),
    }


def reference(x_prompt, x_sample, w_in, attn_sink, attn_out_g, ssm_lambda_re, ssm_lambda_im, ssm_log_dt,
              ssm_b_re, ssm_b_im, ssm_c_re, ssm_c_im, ssm_d, ssm_glu_w, ssm_glu_b, ssm_out_g, w_out,
              ln1_g, ln1_b, ffn_w_up, ffn_conv_w, ffn_conv_b, ffn_w_down, ln2_g, ln2_b):
    params = (w_in, attn_sink, attn_out_g, ssm_lambda_re, ssm_lambda_im, ssm_log_dt, ssm_b_re, ssm_b_im,
              ssm_c_re, ssm_c_im, ssm_d, ssm_glu_w, ssm_glu_b, ssm_out_g, w_out, ln1_g, ln1_b,
              ffn_w_up, ffn_conv_w, ffn_conv_b, ffn_w_down, ln2_g, ln2_b)
    y_prompt = trunk(x_prompt, params)
    y_sample = trunk(x_sample, params)
    return (y_prompt, y_sample)
```

```python
import math
import numpy as np
import concourse.bass as bass
import concourse.mybir as mybir
from concourse.bass_utils import run_bass_kernel_spmd

F32 = mybir.dt.float32
BF16 = mybir.dt.bfloat16
I32 = mybir.dt.int32
AF = mybir.ActivationFunctionType
ALU = mybir.AluOpType
AX = mybir.AxisListType

D = 1024
DEPTH = 4
DFF = 2816
NH = 22
ALPHA = (2 * DEPTH) ** 0.25
EPS = 1e-5
LMAX = 16384
NFM = 25
W1COLS = NFM * 128 + 768
TWO_PI = 2.0 * math.pi
C1 = 6.28125
C2 = TWO_PI - C1


class Trk:
    NDS = 32

    def __init__(self, nc):
        self.nc = nc
        self.E = {'pe': nc.tensor, 'dve': nc.vector, 'act': nc.scalar, 'pool': nc.gpsimd, 'sp': nc.sync}
        self.sid = 0
        self.esem = {}
        self.ecnt = {}
        for e in self.E:
            self.esem[e] = self._newsem(f"e_{e}")
            self.ecnt[e] = 0
        self.dsem = {q: [self._newsem(f"d_{q}{i}") for i in range(self.NDS)] for q in ('sp', 'pool')}
        self.dcnt = {q: [0] * self.NDS for q in ('sp', 'pool')}
        self.dnext = {'sp': 0, 'pool': 0}
        self.lastw = {}
        self.readers = {}
        self.waited = {e: {} for e in self.E}
        self.last = {}
        self.dma_tokens = []
        self.nops = 0

    def _newsem(self, name):
        h = self.nc.alloc_semaphore(name)
        self.sid += 1
        return (self.sid, h)

    def _deps(self, r, w):
        toks = []
        for k in r:
            t = self.lastw.get(k)
            if t is not None:
                toks.append(t)
        for k in w:
            t = self.lastw.get(k)
            if t is not None:
                toks.append(t)
            toks.extend(self.readers.get(k, ()))
        return toks

    def _wait(self, e, toks):
        wd = self.waited[e]
        best = {}
        for (sem, val) in toks:
            if wd.get(sem[0], 0) < val and best.get(sem[0], (None, 0))[1] < val:
                best[sem[0]] = (sem, val)
        for sid, (sem, val) in best.items():
            self.E[e].wait_ge(sem[1], val)
            wd[sid] = val

    def _update(self, r, w, tok):
        for k in r:
            self.readers.setdefault(k, []).append(tok)
        for k in w:
            self.lastw[k] = tok
            self.readers[k] = []

    def op(self, e, fn, r=(), w=()):
        toks = self._deps(r, w)
        if e == 'pe':
            toks = [t for t in toks if t[0][0] != self.esem['pe'][0]]
        self._wait(e, toks)
        ins = fn(self.E[e])
        self.ecnt[e] += 1
        ins.then_inc(self.esem[e][1], 1)
        tok = (self.esem[e], self.ecnt[e])
        self._update(r, w, tok)
        self.last[e] = tok
        self.nops += 1

    def dma(self, q, out, in_, r=(), w=(), **kw):
        toks = self._deps(r, w)
        slot = self.dnext[q]
        self.dnext[q] = (slot + 1) % self.NDS
        sem = self.dsem[q][slot]
        prev = self.dcnt[q][slot]
        if prev > 0:
            toks.append((sem, prev))
        self._wait(q, toks)
        ins = self.E[q].dma_start(out=out, in_=in_, **kw)
        ins.then_inc(sem[1], 16)
        self.dcnt[q][slot] = prev + 16
        tok = (sem, prev + 16)
        self._update(r, w, tok)
        self.dma_tokens.append(tok)
        self.nops += 1

    def barrier(self):
        toks = list(self.last.values()) + self.dma_tokens
        for e in self.E:
            self._wait(e, toks)
        self.dma_tokens = []
        self.lastw = {}
        self.readers = {}


def _perm_w_in():
    cols = []
    QO, KO, VO, UO, RQO, RKO, RVO, RGO = 0, 384, 512, 640, 896, 1280, 1664, 2048

    def head_cols(base, h):
        return [base + 64 * h + d for d in range(64)]

    def swap_att(cs):
        out = list(cs)
        for d in range(8):
            out[d], out[d + 8] = cs[d + 8], cs[d]
        return out

    def swap_ret(cs):
        return cs[32:] + cs[:32]

    qa, qas = [], []
    for g in range(3):
        c0 = head_cols(QO, g)
        c1 = head_cols(QO, 3 + g)
        qa.append(c0 + c1)
        qas.append(swap_att(c0) + swap_att(c1))
    ka = head_cols(KO, 0) + head_cols(KO, 1)
    kas = swap_att(head_cols(KO, 0)) + swap_att(head_cols(KO, 1))
    for g in range(3):
        cols += qa[g]
    for g in range(3):
        cols += qas[g]
    cols += ka + kas
    cols += list(range(UO, UO + 256))
    for base in (RQO, RKO):
        nat, sw = [], []
        for c in range(3):
            h0 = head_cols(base, 2 * c)
            h1 = head_cols(base, 2 * c + 1)
            nat.append(h0 + h1)
            sw.append(swap_ret(h0) + swap_ret(h1))
        for c in range(3):
            cols += nat[c]
        for c in range(3):
            cols += sw[c]
    cols += list(range(RGO, RGO + 384))
    cols += list(range(VO, VO + 128)) + list(range(RVO, RVO + 384)) + list(range(UO, UO + 256))
    assert len(cols) == W1COLS
    return np.array(cols, dtype=np.int64)


def _const_tables(lmax):
    c = {}
    pos = np.arange(lmax, dtype=np.float32)
    inv_a = (np.float32(500000.0) ** (-np.arange(0, 16, 2, dtype=np.float32) / np.float32(16))).astype(np.float32)
    ang_a = (pos[None, :] * inv_a[:, None]).astype(np.float32).astype(np.float64)
    ac = np.ones((128, lmax), np.float32)
    asn = np.zeros((128, lmax), np.float32)
    for p in range(128):
        d = p % 64
        if d < 8:
            ac[p] = np.cos(ang_a[d]); asn[p] = -np.sin(ang_a[d])
        elif d < 16:
            ac[p] = np.cos(ang_a[d - 8]); asn[p] = np.sin(ang_a[d - 8])
    inv_r = (1.0 / (np.float32(10000.0) ** np.linspace(0.0, 1.0, 32, dtype=np.float32))).astype(np.float32)
    ang_r = (pos[None, :] * inv_r[:, None]).astype(np.float32).astype(np.float64)
    rc = np.zeros((128, lmax), np.float32)
    rs = np.zeros((128, lmax), np.float32)
    for p in range(128):
        d = p % 64
        if d < 32:
            rc[p] = np.cos(ang_r[d]); rs[p] = -np.sin(ang_r[d])
        else:
            rc[p] = np.cos(ang_r[d - 32]); rs[p] = np.sin(ang_r[d - 32])
    c['rot'] = np.stack([ac, asn, rc, rs], 0)
    j = np.arange(128)
    mL = (j[:, None] >= j[None, :]).astype(np.float32)
    mU = (j[:, None] <= j[None, :]).astype(np.float32)
    c['mask'] = np.stack([np.tile(mL, (1, 3)), np.tile(mU, (1, 3))], 0)
    logg = np.log(1.0 - 2.0 ** (-5.0 - np.arange(6, dtype=np.float64)))
    C = 128
    DF = np.zeros((128, 384), np.float64); DB = np.zeros((128, 384), np.float64)
    for h in range(6):
        DF[:, 64 * h:64 * h + 64] = np.exp(logg[h] * (C - 1.0 - j))[:, None]
        DB[:, 64 * h:64 * h + 64] = np.exp(logg[h] * j)[:, None]
    c['dfb'] = np.stack([DF, DB], 0).astype(np.float32)
    Gc = np.zeros((128, 3, 64), np.float64)
    QF = np.zeros((128, 3, 128), np.float64); QB = np.zeros((128, 3, 128), np.float64)
    for cc in range(3):
        for hh in range(2):
            h = 2 * cc + hh
            Gc[64 * hh:64 * hh + 64, cc, :] = np.exp(logg[h] * C)
            QF[64 * hh:64 * hh + 64, cc, :] = np.exp(logg[h] * (j + 1.0))[None, :]
            QB[64 * hh:64 * hh + 64, cc, :] = np.exp(logg[h] * (C - j))[None, :]
    c['gc'] = Gc.astype(np.float32)
    c['qfb'] = np.stack([QF, QB], 0).astype(np.float32)
    DECT = np.zeros((128, 6, 128), np.float64)
    for h in range(6):
        DECT[:, h, :] = np.exp(logg[h] * np.abs(j[None, :] - j[:, None]))
    c['dect'] = DECT.astype(np.float32)
    c['ident'] = np.eye(128, dtype=np.float32)
    tri = np.stack([(j[:, None] <= j[None, :]).astype(np.float32),
                    (j[:, None] >= j[None, :]).astype(np.float32)], 0)
    c['tri'] = tri
    ecol = np.zeros((128, 8), np.float32)
    ecol[:, 0] = -j
    ecol[:, 1] = j
    ecol[:, 2] = -(127 - j)
    ecol[:, 3] = 127 - j
    ecol[:, 4] = 1.0
    ecol[:, 5] = 128.0
    c['ecol'] = ecol
    return c


class Builder:
    def __init__(self, seqs, depth, lmax):
        self.seqs = seqs
        self.depth = depth
        self.lmax = lmax
        self.nc = bass.Bass("TRN2", target_bir_lowering=False)
        self.T = Trk(self.nc)
        self.uid = 0

    def din(self, name, shape, dt=F32):
        return self.nc.dram_tensor(name, list(shape), dt, kind="ExternalInput").ap()

    def dout(self, name, shape, dt=F32):
        return self.nc.dram_tensor(name, list(shape), dt, kind="ExternalOutput").ap()

    def dscr(self, name, shape, dt):
        return self.nc.dram_tensor(name, list(shape), dt, kind="Internal").ap()

    def sb(self, name, shape, dt):
        self.uid += 1
        return self.nc.alloc_sbuf_tensor(f"{name}_{self.uid}", list(shape), dt)

    def load(self, out, in_, w, r=(), **kw):
        self.T.dma('sp', out, in_, r=r, w=w, **kw)

    def store(self, out, in_, r, w=(), **kw):
        self.T.dma('pool', out, in_, r=r, w=w, **kw)

    def bank(self, b):
        return (self.PA if b < 4 else self.PB)[:, (b % 4) * 512:(b % 4) * 512 + 512]

    def bankbf(self, b):
        return (self.PAb if b < 4 else self.PBb)[:, (b % 4) * 1024:(b % 4) * 1024 + 1024]

    def load_w(self, dst, src_rows, ncols, key):
        T = self.T
        step = 1024
        for c0 in range(0, ncols, step):
            n = min(step, ncols - c0)
            i = self.wst_i
            self.wst_i ^= 1
            st = self.wst[i]
            self.load(st[:, 0:n], src_rows[:, c0:c0 + n], w=[f"wst{i}"])
            T.op('act', lambda e, st=st, n=n, c0=c0: e.activation(out=dst[:, c0:c0 + n], in_=st[:, 0:n], func=AF.Copy),
                 r=[f"wst{i}"], w=[key])

    def build(self):
        nc, T = self.nc, self.T
        depth = self.depth
        LM = self.lmax
        I = {}
        I['w1'] = self.din("w1", [depth, D, W1COLS])
        I['wo'] = self.din("wo", [depth, D, D])
        I['wu'] = self.din("wu", [depth, D, 2 * DFF])
        I['wd'] = self.din("wd", [depth, DFF, D])
        I['rot'] = self.din("rot", [4, 128, LM])
        I['mask'] = self.din("mask", [2, 128, 384])
        I['dfb'] = self.din("dfb", [2, 128, 384])
        I['gc'] = self.din("gc", [128, 3, 64])
        I['qfb'] = self.din("qfb", [2, 128, 3, 128])
        I['dect'] = self.din("dect", [128, 6, 128])
        I['ident'] = self.din("ident", [128, 128])
        I['tri'] = self.din("tri", [2, 128, 128])
        I['ecol'] = self.din("ecol", [128, 8])
        for nm, shp in [("attn_sink", [depth, 6]), ("attn_out_g", [depth, 384]),
                        ("ssm_lambda_re", [depth, 2, 16, 64]), ("ssm_lambda_im", [depth, 2, 16, 64]),
                        ("ssm_log_dt", [depth, 2, 16]), ("ssm_b_re", [depth, 2, 16, 64, 16]),
                        ("ssm_b_im", [depth, 2, 16, 64, 16]), ("ssm_c_re", [depth, 2, 16, 16, 64]),
                        ("ssm_c_im", [depth, 2, 16, 16, 64]), ("ssm_d", [depth, 256]),
                        ("ssm_glu_w", [depth, 256, 256]), ("ssm_glu_b", [depth, 256]),
                        ("ssm_out_g", [depth, 256]), ("ln1_g", [depth, D]), ("ln1_b", [depth, D]),
                        ("ffn_conv_w", [depth, 3, DFF]), ("ffn_conv_b", [depth, DFF]),
                        ("ln2_g", [depth, D]), ("ln2_b", [depth, D])]:
            I[nm] = self.din(nm, shp)
        self.I = I
        S = []
        for si, L in enumerate(self.seqs):
            s = {'L': L, 'i': si}
            s['xin'] = self.din(f"x{si}", [8, 128, L])
            s['yout'] = self.dout(f"y{si}", [8, 128, L])
            s['xa'] = self.dscr(f"xa{si}", [8, 128, L], F32)
            s['x1'] = self.dscr(f"x1_{si}", [8, 128, L + 2], F32)
            s['qk'] = self.dscr(f"qk{si}", [4, 128, L], BF16)
            s['rqk'] = self.dscr(f"rqk{si}", [6, 128, L], BF16)
            s['ufm'] = self.dscr(f"ufm{si}", [2, 128, L], BF16)
            s['rg'] = self.dscr(f"rg{si}", [3, 128, L], BF16)
            s['vt'] = self.dscr(f"vt{si}", [L, 130], BF16)
            s['rkt'] = self.dscr(f"rkt{si}", [L, 384], BF16)
            s['rvt'] = self.dscr(f"rvt{si}", [L, 384], BF16)
            s['ut'] = self.dscr(f"ut{si}", [L, 256], F32)
            s['yf'] = self.dscr(f"yf{si}", [L, 256], F32)
            s['sf'] = self.dscr(f"sf{si}", [L // 128, 128, 384], BF16)
            s['sbw'] = self.dscr(f"sbw{si}", [L // 128, 128, 384], BF16)
            s['mix'] = self.dscr(f"mix{si}", [8, 128, L], BF16)
            s['hid'] = self.dscr(f"hid{si}", [NH, 128, L], BF16)
            S.append(s)
        self.S = S

        self.PA = nc.alloc_psum_tensor("PA", [128, 2048], F32)
        self.PB = nc.alloc_psum_tensor("PB", [128, 2048], F32)
        self.PAb = self.PA[:].bitcast(BF16)
        self.PBb = self.PB[:].bitcast(BF16)
        self.wbuf = self.sb("wbuf", [128, 8 * 2 * DFF], BF16)
        self.wst = [self.sb("wst0", [128, 1024], F32), self.sb("wst1", [128, 1024], F32)]
        self.wst_i = 0
        self.ident = self.sb("ident", [128, 128], BF16)
        self.ones = self.sb("ones", [128, 128], BF16)
        self.big = self.sb("big", [128, 22016], F32)
        self.wf = self.wbuf[:].bitcast(F32)
        self.woff = 0
        self.bigoff = 0
        cst = self.sb("cst", [128, 128], F32)
        self.load(cst[:], I['ident'][:, :], w=["cst"])
        T.op('act', lambda e: e.activation(out=self.ident[:], in_=cst[:], func=AF.Copy), r=["cst"], w=["ident"])
        T.op('dve', lambda e: e.memset(self.ones[:], 1.0), w=["ones"])
        self.onesrow = self.sb("onesrow", [128, 128], BF16)
        T.op('dve', lambda e: e.memset(self.onesrow[:], 0.0), w=["onesrow"])
        T.op('dve', lambda e: e.memset(self.onesrow[0:1, :], 1.0), w=["onesrow"])
        zt = self.sb("zt", [128, 8], F32)
        T.op('dve', lambda e: e.memset(zt[:], 0.0), w=["zt"])
        for s in S:
            L = s['L']
            with nc.allow_non_contiguous_dma(reason="pad"):
                self.store(s['x1'][:, :, 0:1].rearrange("c p o -> p c o"), zt[:].unsqueeze(2), r=["zt"])
                self.store(s['x1'][:, :, L + 1:L + 2].rearrange("c p o -> p c o"), zt[:].unsqueeze(2), r=["zt"])
        T.barrier()

        import os
        stop = os.environ.get("KSTOP", "")
        for l in range(depth):
            self.l = l
            for nm, fn in [("s1", self.stage1), ("att", self.stage_att), ("ret", self.stage_ret), ("ssm", self.stage_ssm),
                           ("5a", self.stage5a), ("5b", self.stage5b), ("5c", self.stage5c)]:
                if stop == "init":
                    break
                fn()
                if stop == nm:
                    break
            if stop:
                break
        T.barrier()
        return nc

    def reset_big(self):
        self.bigoff = 0
        self.woff = 0

    def carve(self, ncols, dt=F32, region='big'):
        if region == 'w':
            o = self.woff
            self.woff += ncols
            assert self.woff <= 22528, self.woff
            v = self.wf[:, o:o + ncols]
        else:
            o = self.bigoff
            self.bigoff += ncols
            assert self.bigoff <= 22016, self.bigoff
            v = self.big[:, o:o + ncols]
        if dt == BF16:
            v = v.bitcast(BF16)
        return v

    def stage1(self):
        T, I, l = self.T, self.I, self.l
        self.reset_big()
        wb = self.wbuf[:, 0:8 * W1COLS].rearrange("p (k c) -> p k c", k=8)
        for kc in range(8):
            self.load_w(self.wbuf[:, kc * W1COLS:(kc + 1) * W1COLS], I['w1'][l, kc * 128:(kc + 1) * 128, :], W1COLS, "wbuf")
        xf = [self.carve(4096) for _ in range(2)]
        xb = self.carve(2048, BF16)
        tab = [self.carve(2048)] * 2
        tmp1 = [self.carve(512) for _ in range(2)]
        tmp2 = [self.carve(512) for _ in range(2)]
        o1 = self.carve(15 * 256, BF16)
        vts = self.carve(4 * 65, BF16)
        rvs = self.carve(4 * 192, BF16)
        uts = self.carve(4 * 256)
        rks = self.carve(4 * 192, BF16)
        T.op('dve', lambda e: e.memset(vts, 1.0), w=["vts"])
        xbv = xb.rearrange("p (k t) -> p k t", k=8)
        o1v = o1.rearrange("p (c t) -> p c t", c=15)
        jobs = []
        for s in self.S:
            for t0 in range(0, s['L'], 512):
                jobs.append((s, t0))
        src = lambda s: (s['xin'] if l == 0 else s['xa'])

        def ld(j):
            s, t0 = jobs[j]
            p = j % 2
            self.load(xf[p].rearrange("p (k t) -> p k t", k=8), src(s)[:, :, t0:t0 + 512].rearrange("c p t -> p c t"), w=[f"xf{p}"])
        ld(0)
        rot_jobs = [(g, 3 + g, 0, 1.0, g) for g in range(3)] + [(6, 7, 0, 1.0, 3)] + \
                   [(10 + c, 13 + c, 2, 1.0, 4 + c) for c in range(3)] + [(16 + c, 19 + c, 2, 0.125, 7 + c) for c in range(3)]
        for j, (s, t0) in enumerate(jobs):
            p = j % 2
            if j + 1 < len(jobs):
                ld(j + 1)
            self.load(tab[p].rearrange("p (k t) -> p k t", k=4), I['rot'][:, :, t0:t0 + 512].rearrange("c p t -> p c t"), w=["tab"])
            T.op('act', lambda e, p=p: e.activation(out=xb, in_=xf[p], func=AF.Copy), r=[f"xf{p}"], w=["xb"])
            tv = tab[p].rearrange("p (k t) -> p k t", k=4)

            def fm_mm(chunk, b):
                def f(e):
                    for kc in range(8):
                        ins = e.matmul(self.bank(b), wb[:, kc, chunk * 128:(chunk + 1) * 128], xbv[:, kc, :],
                                       start=(kc == 0), stop=(kc == 7))
                    return ins
                T.op('pe', f, r=["wbuf", "xb"], w=[("ps", b)])
            for ri, (cm, cs, tb, sc, oc) in enumerate(rot_jobs):
                q = ri % 2
                b0, b1 = 2 * q, 2 * q + 1
                fm_mm(cm, b0)
                fm_mm(cs, b1)
                T.op('dve', lambda e, b0=b0, q=q, tb=tb, sc=sc: e.scalar_tensor_tensor(
                    out=tmp1[q], in0=self.bank(b0), scalar=sc, in1=tv[:, tb, :], op0=ALU.mult, op1=ALU.mult),
                    r=[("ps", b0), "tab"], w=[f"tmp1{q}"])
                T.op('dve', lambda e, b1=b1, q=q, tb=tb, sc=sc: e.scalar_tensor_tensor(
                    out=tmp2[q], in0=self.bank(b1), scalar=sc, in1=tv[:, tb + 1, :], op0=ALU.mult, op1=ALU.mult),
                    r=[("ps", b1), "tab"], w=[f"tmp2{q}"])
                T.op('dve', lambda e, q=q, oc=oc: e.tensor_tensor(out=o1v[:, oc, :], in0=tmp1[q], in1=tmp2[q], op=ALU.add),
                     r=[f"tmp1{q}", f"tmp2{q}"], w=["o1"])
            for pi, (cm, oc, fn) in enumerate([(8, 10, AF.Copy), (9, 11, AF.Copy), (22, 12, AF.Silu), (23, 13, AF.Silu), (24, 14, AF.Silu)]):
                b = 4 + pi % 2
                fm_mm(cm, b)
                T.op('act', lambda e, b=b, oc=oc, fn=fn: e.activation(out=o1v[:, oc, :], in_=self.bank(b), func=fn),
                     r=[("ps", b)], w=["o1"])
            for bl in range(4):
                def f(e, bl=bl):
                    for kc in range(8):
                        ins = e.matmul(self.bank(6), xbv[:, kc, bl * 128:(bl + 1) * 128], wb[:, kc, 3200:3712],
                                       start=(kc == 0), stop=(kc == 7))
                    for kc in range(8):
                        ins = e.matmul(self.bank(7)[:, 0:256], xbv[:, kc, bl * 128:(bl + 1) * 128], wb[:, kc, 3712:3968],
                                       start=(kc == 0), stop=(kc == 7))
                    return ins
                T.op('pe', f, r=["wbuf", "xb"], w=[("ps", 6), ("ps", 7)])
                vv = vts.rearrange("p (b k e) -> p b k e", b=4, k=2)
                T.op('act', lambda e, bl=bl, vv=vv: e.activation(out=vv[:, bl, :, 0:64], in_=self.bank(6)[:, 0:128].rearrange("p (k e) -> p k e", k=2), func=AF.Copy),
                     r=[("ps", 6)], w=["vts"])
                T.op('act', lambda e, bl=bl: e.activation(out=rvs[:, bl * 384:(bl + 1) * 384], in_=self.bank(6)[:, 128:512], func=AF.Copy),
                     r=[("ps", 6)], w=["rvs"])
                T.op('act', lambda e, bl=bl: e.activation(out=uts[:, bl * 256:(bl + 1) * 256], in_=self.bank(7)[:, 0:256], func=AF.Copy),
                     r=[("ps", 7)], w=["uts"])
            def ftr(e):
                for bl in range(4):
                    for c in range(3):
                        k = bl * 3 + c
                        ins = e.transpose(self.PBb[:, k * 128:(k + 1) * 128], o1v[:, 7 + c, bl * 128:(bl + 1) * 128], self.ident[:])
                return ins
            T.op('pe', ftr, r=["o1", "ident"], w=[("ps", 4), ("ps", 5)])
            T.op('act', lambda e: e.activation(out=rks, in_=self.PBb[:, 0:1536], func=AF.Copy),
                 r=[("ps", 4), ("ps", 5)], w=["rks"])
            for (dst, c0, n) in [(s['qk'], 0, 4), (s['rqk'], 4, 6), (s['ufm'], 10, 2), (s['rg'], 12, 3)]:
                self.store(dst[:, :, t0:t0 + 512].rearrange("c p t -> p c t"), o1v[:, c0:c0 + n, :], r=["o1"])
            self.store(s['vt'][t0:t0 + 512, :].rearrange("(b p) f -> p b f", p=128), vts.rearrange("p (b f) -> p b f", b=4), r=["vts"])
            self.store(s['rvt'][t0:t0 + 512, :].rearrange("(b p) f -> p b f", p=128), rvs.rearrange("p (b f) -> p b f", b=4), r=["rvs"])
            self.store(s['ut'][t0:t0 + 512, :].rearrange("(b p) f -> p b f", p=128), uts.rearrange("p (b f) -> p b f", b=4), r=["uts"])
            self.store(s['rkt'][t0:t0 + 512, :].rearrange("(b p) f -> p b f", p=128), rks.rearrange("p (b f) -> p b f", b=4), r=["rks"])
        T.barrier()

    def stage_att(self):
        T, I, l = self.T, self.I, self.l
        self.reset_big()
        mask = self.carve(384, BF16)
        mst = self.carve(768)
        gt = self.carve(384)
        esk = self.carve(8)
        for m in range(2):
            self.load(mst[:, m * 384:(m + 1) * 384], I['mask'][m], w=["mst"])
        T.op('act', lambda e: e.activation(out=mask, in_=mst, func=AF.Copy), r=["mst"], w=["mask"])
        self.load(gt, I['attn_out_g'][l:l + 1, :].broadcast_to([128, 384]), w=["gt"])
        self.load(esk[:, 0:6], I['attn_sink'][l:l + 1, :].broadcast_to([128, 6]), w=["esk"])
        T.op('act', lambda e: e.activation(out=esk[:, 0:6], in_=esk[:, 0:6], func=AF.Exp), r=["esk"], w=["esk"])
        qs = [self.carve(192, BF16) for _ in range(2)]
        ks = [self.carve(192, BF16) for _ in range(2)]
        vs = [self.carve(195, BF16) for _ in range(2)]
        pT = self.carve(3 * 192, BF16)
        osb = self.carve(384)
        sq = self.carve(384)
        den = self.carve(8); rec = self.carve(8); ss = self.carve(8)
        an = self.carve(192, BF16)
        mo = self.carve(192, BF16)
        jobs = []
        for s in self.S:
            nb = s['L'] // 128
            for n in range(nb):
                jobs.append((s, n, nb))

        def ld(j):
            s, n, nb = jobs[j]
            p = j % 2
            lo, hi = max(n - 1, 0), min(n + 1, nb - 1)
            w0 = (lo - (n - 1))
            self.load(qs[p].rearrange("p (c t) -> p c t", c=3), s['qk'][0:3, :, n * 128:(n + 1) * 128].rearrange("c p t -> p c t"), w=[f"qs{p}"])
            self.load(ks[p][:, w0 * 128:(w0 + hi - lo + 1) * 128], s['qk'][3, :, lo * 128:(hi + 1) * 128], w=[f"ks{p}"])
            self.load(vs[p].rearrange("p (b f) -> p b f", b=3)[:, w0:w0 + hi - lo + 1, :],
                      s['vt'][lo * 128:(hi + 1) * 128, :].rearrange("(b p) f -> p b f", p=128), w=[f"vs{p}"])
        ld(0)
        for j, (s, n, nb) in enumerate(jobs):
            p = j % 2
            if j + 1 < len(jobs):
                ld(j + 1)
            ms = [m for m in (-1, 0, 1) if 0 <= n + m < nb]
            qv = qs[p].rearrange("p (c t) -> p c t", c=3)
            vv = vs[p].rearrange("p (b k e) -> p b k e", b=3, k=2)
            for kv in range(2):
                pr = slice(64 * kv, 64 * kv + 64)
                for mi, m in enumerate(ms):
                    b = mi % 3
                    T.op('pe', lambda e, b=b, m=m, pr=pr: e.matmul(self.bank(b)[:, 0:384], ks[p][pr, (m + 1) * 128:(m + 2) * 128], qv[pr, :, :], start=True, stop=True),
                         r=[f"ks{p}", f"qs{p}"], w=[("ps", b)])
                    T.op('act', lambda e, b=b, m=m: e.activation(out=pT[:, (m + 1) * 384:(m + 2) * 384], in_=self.bank(b)[:, 0:384], func=AF.Exp, scale=0.125),
                         r=[("ps", b)], w=[("pT", m)])
                    if m != 0:
                        mk = mask[:, 0:384] if m == -1 else mask[:, 384:768]
                        T.op('dve', lambda e, m=m, mk=mk: e.tensor_tensor(out=pT[:, (m + 1) * 384:(m + 2) * 384], in0=pT[:, (m + 1) * 384:(m + 2) * 384], in1=mk, op=ALU.mult),
                             r=[("pT", m), "mask"], w=[("pT", m)])

                def fpv(e, kv=kv):
                    for g in range(3):
                        h = kv * 3 + g
                        for mi, m in enumerate(ms):
                            ins = e.matmul(self.bank(4)[:, h * 65:(h + 1) * 65], pT[:, (m + 1) * 384 + g * 128:(m + 1) * 384 + (g + 1) * 128],
                                           vv[:, m + 1, kv, :], start=(mi == 0), stop=(mi == len(ms) - 1))
                    return ins
                T.op('pe', fpv, r=[("pT", -1), ("pT", 0), ("pT", 1), f"vs{p}"], w=[("ps", 4)])
            ov = self.bank(4)[:, 0:390].rearrange("p (h e) -> p h e", h=6)
            T.op('dve', lambda e: e.tensor_tensor(out=den[:, 0:6].unsqueeze(2), in0=ov[:, :, 64:65], in1=esk[:, 0:6].unsqueeze(2), op=ALU.add), r=[("ps", 4), "esk"], w=["den"])
            T.op('dve', lambda e: e.reciprocal(rec[:, 0:6], den[:, 0:6]), r=["den"], w=["rec"])
            T.op('dve', lambda e: e.tensor_tensor(out=osb.rearrange("p (h e) -> p h e", h=6), in0=ov[:, :, 0:64],
                                                  in1=rec[:, 0:6].unsqueeze(2).broadcast_to([128, 6, 64]), op=ALU.mult),
                 r=[("ps", 4), "rec"], w=["osb"])
            T.op('act', lambda e: e.activation(out=sq, in_=osb, func=AF.Square), r=["osb"], w=["sq"])
            T.op('dve', lambda e: e.tensor_reduce(out=ss[:, 0:1], in_=sq, axis=AX.X, op=ALU.add), r=["sq"], w=["ss"])
            T.op('act', lambda e: e.activation(out=ss[:, 0:1], in_=ss[:, 0:1], func=AF.Sqrt, scale=1.0 / 384, bias=EPS), r=["ss"], w=["ss"])
            T.op('dve', lambda e: e.reciprocal(ss[:, 1:2], ss[:, 0:1]), r=["ss"], w=["ss"])
            T.op('dve', lambda e: e.scalar_tensor_tensor(out=an, in0=osb, scalar=ss[:, 1:2], in1=gt, op0=ALU.mult, op1=ALU.mult),
                 r=["osb", "ss", "gt"], w=["an"])

            def ftr(e):
                for c in range(3):
                    ins = e.transpose(self.PBb[:, 1024 + c * 128:1024 + (c + 1) * 128], an[:, c * 128:(c + 1) * 128], self.ident[:])
                return ins
            T.op('pe', ftr, r=["an", "ident"], w=[("ps", 5)])
            T.op('act', lambda e: e.activation(out=mo, in_=self.PBb[:, 1024:1024 + 384], func=AF.Copy), r=[("ps", 5)], w=["mo"])
            self.store(s['mix'][0:3, :, n * 128:(n + 1) * 128].rearrange("c p t -> p c t"), mo.rearrange("p (c t) -> p c t", c=3), r=["mo"])
        T.barrier()

    def stage_ret(self):
        T, I, l = self.T, self.I, self.l
        self.reset_big()
        dfb = self.carve(768); gc = self.carve(192); qfb = self.carve(768); dect = self.carve(768)
        self.load(dfb.rearrange("p (z f) -> p z f", z=2), I['dfb'].rearrange("z p f -> p z f"), w=["dfb"])
        self.load(gc, I['gc'].rearrange("p c e -> p (c e)"), w=["gc"])
        self.load(qfb.rearrange("p (z f) -> p z f", z=2), I['qfb'].rearrange("z p c i -> p z (c i)"), w=["qfb"])
        self.load(dect, I['dect'].rearrange("p h i -> p (h i)"), w=["dect"])
        kt = [self.carve(192, BF16) for _ in range(2)]
        vt = [self.carve(192, BF16) for _ in range(2)]
        kf = self.carve(192, BF16)
        St = self.carve(192)
        tmp = self.carve(192)
        Sb = self.carve(192, BF16)
        T.op('dve', lambda e: e.memset(Sb, 0.0), w=["Sb"])
        for z in range(2):
            for s in self.S:
                nch = s['L'] // 128
                order = list(range(nch)) if z == 0 else list(range(nch - 1, -1, -1))
                dst = s['sf'] if z == 0 else s['sbw']
                T.op('dve', lambda e: e.memset(St, 0.0), w=["St"])

                def ld(i):
                    n = order[i]; p = i % 2
                    self.load(kt[p], s['rkt'][n * 128:(n + 1) * 128, :], w=[f"kt{p}"])
                    self.load(vt[p], s['rvt'][n * 128:(n + 1) * 128, :], w=[f"vt{p}"])
                ld(0)
                for i, n in enumerate(order):
                    p = i % 2
                    if i + 1 < nch:
                        ld(i + 1)
                    T.op('act', lambda e: e.activation(out=Sb[0:64, 0:192], in_=St[0:64, :], func=AF.Copy), r=["St"], w=["Sb"])
                    T.op('act', lambda e: e.activation(out=Sb[64:128, 192:384], in_=St[64:128, :], func=AF.Copy), r=["St"], w=["Sb"])
                    self.store(dst[n], Sb, r=["Sb"])
                    T.op('dve', lambda e, p=p: e.tensor_tensor(out=kf, in0=kt[p], in1=dfb[:, z * 384:(z + 1) * 384], op=ALU.mult), r=[f"kt{p}", "dfb"], w=["kf"])

                    def fr(e, p=p):
                        for c in range(3):
                            ins = e.matmul(self.bank(0)[:, c * 128:(c + 1) * 128], kf[:, c * 128:(c + 1) * 128], vt[p][:, c * 128:(c + 1) * 128], start=True, stop=True)
                        return ins
                    T.op('pe', fr, r=["kf", f"vt{p}"], w=[("ps", 0)])
                    T.op('dve', lambda e: e.tensor_tensor(out=tmp, in0=St, in1=gc, op=ALU.mult), r=["St", "gc"], w=["tmp"])
                    pv = self.bank(0)[:, 0:384].rearrange("p (c x) -> p c x", c=3)
                    Sv = St.rearrange("p (c e) -> p c e", c=3); tv = tmp.rearrange("p (c e) -> p c e", c=3)
                    T.op('dve', lambda e: e.tensor_tensor(out=Sv[0:64], in0=tv[0:64], in1=pv[0:64, :, 0:64], op=ALU.add), r=["tmp", ("ps", 0)], w=["St"])
                    T.op('dve', lambda e: e.tensor_tensor(out=Sv[64:128], in0=tv[64:128], in1=pv[64:128, :, 64:128], op=ALU.add), r=["tmp", ("ps", 0)], w=["St"])
            T.barrier()
        qk = [self.carve(384, BF16) for _ in range(2)]
        sfb = [self.carve(384, BF16) for _ in range(2)]
        rgt = [self.carve(192, BF16) for _ in range(2)]
        qf = self.carve(192, BF16); qb = self.carve(192, BF16)
        pT = self.carve(384, BF16)
        osb = self.carve(384); sq = self.carve(384); xc = self.carve(384)
        st1 = self.carve(8); st2 = self.carve(8); st3 = self.carve(8)
        rn = self.carve(192, BF16); mo = self.carve(192, BF16)
        jobs = []
        for s in self.S:
            for n in range(s['L'] // 128):
                jobs.append((s, n))

        def ld(j):
            s, n = jobs[j]; p = j % 2
            self.load(qk[p].rearrange("p (c t) -> p c t", c=6), s['rqk'][:, :, n * 128:(n + 1) * 128].rearrange("c p t -> p c t"), w=[f"qk{p}"])
            self.load(vt[p], s['rvt'][n * 128:(n + 1) * 128, :], w=[f"vt{p}"])
            self.load(sfb[p][:, 0:384], s['sf'][n], w=[f"sfb{p}"])
            self.load(sfb[p][:, 384:768], s['sbw'][n], w=[f"sfb{p}"])
            self.load(rgt[p].rearrange("p (c t) -> p c t", c=3), s['rg'][:, :, n * 128:(n + 1) * 128].rearrange("c p t -> p c t"), w=[f"rgt{p}"])
        ld(0)
        for j, (s, n) in enumerate(jobs):
            p = j % 2
            if j + 1 < len(jobs):
                ld(j + 1)
            qv = qk[p].rearrange("p (c t) -> p c t", c=6)
            T.op('dve', lambda e: e.tensor_tensor(out=qf, in0=qk[p][:, 0:384], in1=qfb[:, 0:384], op=ALU.mult), r=[f"qk{p}", "qfb"], w=["qf"])
            T.op('dve', lambda e: e.tensor_tensor(out=qb, in0=qk[p][:, 0:384], in1=qfb[:, 384:768], op=ALU.mult), r=[f"qk{p}", "qfb"], w=["qb"])

            def fs(e):
                for hh in range(2):
                    pr = slice(64 * hh, 64 * hh + 64)
                    for c in range(3):
                        ins = e.matmul(self.bank(hh)[:, c * 128:(c + 1) * 128], qv[pr, 3 + c, :], qv[pr, c, :], start=True, stop=True)
                return ins
            T.op('pe', fs, r=[f"qk{p}"], w=[("ps", 0), ("ps", 1)])
            dv = dect.rearrange("p (c hh i) -> p hh c i", c=3, hh=2)
            for hh in range(2):
                T.op('dve', lambda e, hh=hh: e.tensor_tensor(out=pT[:, hh * 384:(hh + 1) * 384].rearrange("p (c i) -> p c i", c=3),
                                                             in0=self.bank(hh)[:, 0:384].rearrange("p (c i) -> p c i", c=3), in1=dv[:, hh], op=ALU.mult),
                     r=[("ps", hh), "dect"], w=["pTa" if hh == 0 else "pTb"])

            def fo(e):
                for h in range(6):
                    c, hh = h // 2, h % 2
                    out = self.bank(2)[:, h * 64:(h + 1) * 64]
                    e.matmul(out, pT[:, hh * 384 + c * 128:hh * 384 + (c + 1) * 128], vt[p][:, h * 64:(h + 1) * 64], start=True, stop=False)
                    e.matmul(out, qf[:, c * 128:(c + 1) * 128], sfb[p][:, hh * 192 + c * 64:hh * 192 + (c + 1) * 64], start=False, stop=False)
                    ins = e.matmul(out, qb[:, c * 128:(c + 1) * 128], sfb[p][:, 384 + hh * 192 + c * 64:384 + hh * 192 + (c + 1) * 64], start=False, stop=True)
                return ins
            T.op('pe', fo, r=["pTa", "pTb", f"vt{p}", "qf", "qb", f"sfb{p}"], w=[("ps", 2)])
            o3 = osb.rearrange("p (h e) -> p h e", h=6)
            T.op('act', lambda e: e.activation(out=osb, in_=self.bank(2)[:, 0:384], func=AF.Copy), r=[("ps", 2)], w=["osb"])
            T.op('act', lambda e: e.activation(out=sq, in_=self.bank(2)[:, 0:384], func=AF.Square), r=[("ps", 2)], w=["sq"])
            T.op('dve', lambda e: e.tensor_reduce(out=st1[:, 0:6], in_=o3, axis=AX.X, op=ALU.add), r=["osb"], w=["st1"])
            T.op('dve', lambda e: e.tensor_reduce(out=st2[:, 0:6], in_=sq.rearrange("p (h e) -> p h e", h=6), axis=AX.X, op=ALU.add), r=["sq"], w=["st2"])
            T.op('dve', lambda e: e.tensor_scalar(out=st1[:, 0:6], in0=st1[:, 0:6], scalar1=1.0 / 64, scalar2=None, op0=ALU.mult), r=["st1"], w=["st1"])
            T.op('dve', lambda e: e.tensor_tensor(out=st3[:, 0:6], in0=st1[:, 0:6], in1=st1[:, 0:6], op=ALU.mult), r=["st1"], w=["st3"])
            T.op('dve', lambda e: e.scalar_tensor_tensor(out=st2[:, 0:6], in0=st2[:, 0:6], scalar=1.0 / 64, in1=st3[:, 0:6], op0=ALU.mult, op1=ALU.subtract), r=["st2", "st3"], w=["st2"])
            T.op('act', lambda e: e.activation(out=st2[:, 0:6], in_=st2[:, 0:6], func=AF.Sqrt, bias=EPS), r=["st2"], w=["st2"])
            T.op('dve', lambda e: e.reciprocal(st3[:, 0:6], st2[:, 0:6]), r=["st2"], w=["st3"])
            T.op('dve', lambda e: e.tensor_tensor(out=xc.rearrange("p (h e) -> p h e", h=6), in0=o3, in1=st1[:, 0:6].unsqueeze(2).broadcast_to([128, 6, 64]), op=ALU.subtract),
                 r=["osb", "st1"], w=["xc"])
            T.op('dve', lambda e: e.tensor_tensor(out=rn.rearrange("p (h e) -> p h e", h=6), in0=xc.rearrange("p (h e) -> p h e", h=6),
                                                  in1=st3[:, 0:6].unsqueeze(2).broadcast_to([128, 6, 64]), op=ALU.mult), r=["xc", "st3"], w=["rn"])

            def ftr(e):
                for c in range(3):
                    ins = e.transpose(self.PBb[:, 1024 + c * 128:1024 + (c + 1) * 128], rn[:, c * 128:(c + 1) * 128], self.ident[:])
                return ins
            T.op('pe', ftr, r=["rn", "ident"], w=[("ps", 5)])
            T.op('dve', lambda e: e.tensor_tensor(out=mo, in0=self.PBb[:, 1024:1024 + 384], in1=rgt[p], op=ALU.mult), r=[("ps", 5), f"rgt{p}"], w=["mo"])
            self.store(s['mix'][5:8, :, n * 128:(n + 1) * 128].rearrange("c p t -> p c t"), mo.rearrange("p (c t) -> p c t", c=3), r=["mo"])
        T.barrier()

    def stage_ssm(self):
        nc, T, I, l = self.nc, self.T, self.I, self.l
        self.reset_big()
        G, P = 16, 64
        NG = G * P
        ecol = self.carve(8, region='w')
        self.load(ecol, I['ecol'][:, :], w=["ecol"])
        lre = self.carve(NG, region='w'); lim = self.carve(NG, region='w'); dt = self.carve(16, region='w')
        rho = self.carve(NG, region='w'); phi = self.carve(NG, region='w')
        a1 = self.carve(NG, region='w'); a2 = self.carve(NG, region='w'); a3 = self.carve(NG, region='w'); a4 = self.carve(NG, region='w'); a5 = self.carve(NG, region='w')
        ki = self.carve(NG, region='w')
        kiv = ki.bitcast(I32)
        tabs = {}
        for z in range(2):
            for nm in ('nr', 'ni', 'pr', 'pi'):
                tabs[(z, nm)] = self.carve(NG)
        l128 = self.carve(4 * NG, region='w')
        cr = self.carve(NG, region='w'); ci = self.carve(NG, region='w')
        bst = self.carve(2048, region='w')
        bblk = [self.carve(1024, BF16, region='w') for _ in range(2)]
        cst = self.carve(256, region='w')
        cblk = [self.carve(128, BF16, region='w') for _ in range(2)]
        trif = self.carve(256, region='w')
        tri = self.carve(128, BF16, region='w')
        gw = self.carve(256, BF16, region='w')
        drow = self.carve(256, region='w'); gbrow = self.carve(256, region='w'); sgrow = self.carve(256, region='w')

        def v3(a):
            return a.rearrange("p (g q) -> p g q", g=G)

        def power_table(z, ec, outr, outi, tag):
            es = ecol[:, ec:ec + 1]
            T.op('act', lambda e: e.activation(out=a1, in_=rho, func=AF.Exp, scale=es), r=["rho", "ecol"], w=["a1"])
            T.op('dve', lambda e: e.tensor_scalar(out=a2, in0=phi, scalar1=es, scalar2=None, op0=ALU.mult), r=["phi", "ecol"], w=["a2"])
            T.op('dve', lambda e: e.tensor_scalar(out=kiv, in0=a2, scalar1=1.0 / TWO_PI, scalar2=None, op0=ALU.mult), r=["a2"], w=["ki"])
            T.op('dve', lambda e: e.tensor_copy(a3, kiv), r=["ki"], w=["a3"])
            T.op('dve', lambda e: e.scalar_tensor_tensor(out=a2, in0=a3, scalar=-C1, in1=a2, op0=ALU.mult, op1=ALU.add), r=["a3", "a2"], w=["a2"])
            T.op('dve', lambda e: e.scalar_tensor_tensor(out=a2, in0=a3, scalar=-C2, in1=a2, op0=ALU.mult, op1=ALU.add), r=["a3", "a2"], w=["a2"])
            T.op('act', lambda e: e.activation(out=a3, in_=a2, func=AF.Sin, scale=0.5), r=["a2"], w=["a3"])
            T.op('act', lambda e: e.activation(out=a4, in_=a2, func=AF.Abs, scale=0.5), r=["a2"], w=["a4"])
            T.op('act', lambda e: e.activation(out=a4, in_=a4, func=AF.Sin, scale=-1.0, bias=math.pi / 2), r=["a4"], w=["a4"])
            T.op('dve', lambda e: e.tensor_tensor(out=a5, in0=a3, in1=a4, op=ALU.mult), r=["a3", "a4"], w=["a5"])
            T.op('dve', lambda e: e.scalar_tensor_tensor(out=outi, in0=a5, scalar=2.0, in1=a1, op0=ALU.mult, op1=ALU.mult), r=["a5", "a1"], w=[tag + "i"])
            T.op('dve', lambda e: e.tensor_tensor(out=a5, in0=a3, in1=a3, op=ALU.mult), r=["a3"], w=["a5"])
            T.op('dve', lambda e: e.tensor_scalar(out=a5, in0=a5, scalar1=-2.0, scalar2=1.0, op0=ALU.mult, op1=ALU.add), r=["a5"], w=["a5"])
            T.op('dve', lambda e: e.tensor_tensor(out=outr, in0=a5, in1=a1, op=ALU.mult), r=["a5", "a1"], w=[tag + "r"])

        self.load(trif.rearrange("p (z t) -> p z t", z=2), I['tri'].rearrange("z p t -> p z t"), w=["trif"])
        T.op('act', lambda e: e.activation(out=tri, in_=trif, func=AF.Copy), r=["trif"], w=["tri"])
        for kc in range(2):
            self.load_w(gw[:, kc * 256:(kc + 1) * 256], I['ssm_glu_w'][l, kc * 128:(kc + 1) * 128, :], 256, "gw")
        self.load(drow, I['ssm_d'][l:l + 1, :].broadcast_to([128, 256]), w=["drow"])
        self.load(gbrow, I['ssm_glu_b'][l:l + 1, :].broadcast_to([128, 256]), w=["gbrow"])
        self.load(sgrow, I['ssm_out_g'][l:l + 1, :].broadcast_to([128, 256]), w=["sgrow"])
        for z in range(2):
            self.load(lre, I['ssm_lambda_re'][l, z:z + 1].rearrange("o g p -> o (g p)").broadcast_to([128, NG]), w=["lre"])
            self.load(lim, I['ssm_lambda_im'][l, z:z + 1].rearrange("o g p -> o (g p)").broadcast_to([128, NG]), w=["lim"])
            self.load(dt, I['ssm_log_dt'][l, z:z + 1, :].broadcast_to([128, 16]), w=["dt"])
            T.op('act', lambda e: e.activation(out=dt, in_=dt, func=AF.Exp), r=["dt"], w=["dt"])
            dtb = dt.unsqueeze(2).broadcast_to([128, G, P])
            T.op('dve', lambda e: e.tensor_tensor(out=v3(rho), in0=v3(lre), in1=dtb, op=ALU.mult), r=["lre", "dt"], w=["rho"])
            T.op('dve', lambda e: e.tensor_tensor(out=v3(phi), in0=v3(lim), in1=dtb, op=ALU.mult), r=["lim", "dt"], w=["phi"])
            power_table(z, 4, cr, ci, "cf")
            T.op('dve', lambda e: e.tensor_scalar(out=cr, in0=cr, scalar1=-1.0, scalar2=None, op0=ALU.add), r=["cfr"], w=["cfr"])
            T.op('dve', lambda e: e.tensor_tensor(out=a1, in0=lre, in1=lre, op=ALU.mult), r=["lre"], w=["a1"])
            T.op('dve', lambda e: e.tensor_tensor(out=a2, in0=lim, in1=lim, op=ALU.mult), r=["lim"], w=["a2"])
            T.op('dve', lambda e: e.tensor_tensor(out=a1, in0=a1, in1=a2, op=ALU.add), r=["a1", "a2"], w=["a1"])
            T.op('dve', lambda e: e.reciprocal(a1, a1), r=["a1"], w=["a1"])
            T.op('dve', lambda e: e.tensor_tensor(out=a2, in0=cr, in1=lre, op=ALU.mult), r=["cfr", "lre"], w=["a2"])
            T.op('dve', lambda e: e.tensor_tensor(out=a3, in0=ci, in1=lim, op=ALU.mult), r=["cfi", "lim"], w=["a3"])
            T.op('dve', lambda e: e.tensor_tensor(out=a2, in0=a2, in1=a3, op=ALU.add), r=["a2", "a3"], w=["a2"])
            T.op('dve', lambda e: e.tensor_tensor(out=a3, in0=ci, in1=lre, op=ALU.mult), r=["cfi", "lre"], w=["a3"])
            T.op('dve', lambda e: e.tensor_tensor(out=a4, in0=cr, in1=lim, op=ALU.mult), r=["cfr", "lim"], w=["a4"])
            T.op('dve', lambda e: e.tensor_tensor(out=a3, in0=a3, in1=a4, op=ALU.subtract), r=["a3", "a4"], w=["a3"])
            T.op('dve', lambda e: e.tensor_tensor(out=cr, in0=a2, in1=a1, op=ALU.mult), r=["a2", "a1"], w=["cfr"])
            T.op('dve', lambda e: e.tensor_tensor(out=ci, in0=a3, in1=a1, op=ALU.mult), r=["a3", "a1"], w=["cfi"])
            nr, ni, pr_, pi_ = tabs[(z, 'nr')], tabs[(z, 'ni')], tabs[(z, 'pr')], tabs[(z, 'pi')]
            power_table(z, 0 if z == 0 else 2, pr_, pi_, f"t{z}p")
            T.op('dve', lambda e: e.tensor_tensor(out=a2, in0=pr_, in1=cr, op=ALU.mult), r=[f"t{z}pr", "cfr"], w=["a2"])
            T.op('dve', lambda e: e.tensor_tensor(out=a3, in0=pi_, in1=ci, op=ALU.mult), r=[f"t{z}pi", "cfi"], w=["a3"])
            T.op('dve', lambda e: e.tensor_tensor(out=nr, in0=a2, in1=a3, op=ALU.subtract), r=["a2", "a3"], w=[f"t{z}nr"])
            T.op('dve', lambda e: e.tensor_tensor(out=a2, in0=pr_, in1=ci, op=ALU.mult), r=[f"t{z}pr", "cfi"], w=["a2"])
            T.op('dve', lambda e: e.tensor_tensor(out=a3, in0=pi_, in1=cr, op=ALU.mult), r=[f"t{z}pi", "cfr"], w=["a3"])
            T.op('dve', lambda e: e.tensor_tensor(out=ni, in0=a2, in1=a3, op=ALU.add), r=["a2", "a3"], w=[f"t{z}ni"])
            power_table(z, 5, l128[:, (2 * z) * NG:(2 * z + 1) * NG], l128[:, (2 * z + 1) * NG:(2 * z + 2) * NG], f"l128{z}")
            power_table(z, 1 if z == 0 else 3, pr_, pi_, f"t{z}p")
            T.op('dve', lambda e: e.memset(bst, 0.0), w=["bst"])
            with nc.allow_non_contiguous_dma(reason="small param transpose"):
                for g in range(G):
                    cc, g8 = g // 8, g % 8
                    for ri, nm in enumerate(("ssm_b_re", "ssm_b_im")):
                        o = cc * 1024 + g8 * 128 + ri * 64
                        self.load(bst[16 * g8:16 * g8 + 16, o:o + 64], I[nm][l, z, g].rearrange("p h -> h p"), w=["bst"])
                for ri, nm in enumerate(("ssm_c_re", "ssm_c_im")):
                    self.load(cst[64 * ri:64 * ri + 64, :].rearrange("p (g h) -> p g h", g=G), I[nm][l, z].rearrange("g h p -> p g h"), w=["cst"])
            T.op('act', lambda e, z=z: e.activation(out=bblk[z], in_=bst, func=AF.Copy), r=["bst"], w=[f"bblk{z}"])
            T.op('act', lambda e, z=z: e.activation(out=cblk[z], in_=cst, func=AF.Copy), r=["cst"], w=[f"cblk{z}"])

        uT = [self.carve(128, BF16) for _ in range(2)]
        Kt = self.carve(1024, BF16)
        t1 = self.carve(NG); t2 = self.carve(NG)
        X = self.carve(1024, BF16)
        XT = self.carve(1024, BF16)
        srow = self.carve(2048); xin = self.carve(2048)
        xhi = self.carve(1024, BF16); xlo = self.carve(1024, BF16)
        yfl = [self.carve(256) for _ in range(2)]
        utl = [self.carve(256) for _ in range(2)]
        g1 = self.carve(256); g2 = self.carve(256)
        ysb = g1
        sbf = self.carve(128, BF16); sT = self.carve(128, BF16)
        sn = self.carve(128, BF16); mo = self.carve(128, BF16)
        s8 = self.carve(8)

        def cview(ap, ri):
            return ap.rearrange("p (g r q) -> p g r q", g=G, r=2)[:, :, ri, :]

        for z in range(2):
            nr, ni, pr_, pi_ = (v3(tabs[(z, k)]) for k in ('nr', 'ni', 'pr', 'pi'))
            Lr = l128[0:1, (2 * z) * NG:(2 * z + 1) * NG]
            Li = l128[0:1, (2 * z + 1) * NG:(2 * z + 2) * NG]
            for s in self.S:
                nch = s['L'] // 128
                order = list(range(nch)) if z == 0 else list(range(nch - 1, -1, -1))
                T.op('dve', lambda e: e.memset(xhi, 0.0), w=["xhi"])
                T.op('dve', lambda e: e.memset(xlo, 0.0), w=["xlo"])

                def ld(i):
                    n = order[i]; p = i % 2
                    self.load(uT[p].rearrange("p (c t) -> p c t", c=2), s['ufm'][:, :, n * 128:(n + 1) * 128].rearrange("c p t -> p c t"), w=[f"uT{p}"])
                    if z == 1:
                        self.load(yfl[p], s['yf'][n * 128:(n + 1) * 128, :], w=[f"yfl{p}"])
                        self.load(utl[p], s['ut'][n * 128:(n + 1) * 128, :], w=[f"utl{p}"])
                ld(0)
                for i, n in enumerate(order):
                    p = i % 2
                    if i + 1 < nch:
                        ld(i + 1)
                    uv = uT[p].rearrange("p (c t) -> p c t", c=2)
                    bv = bblk[z].rearrange("p (c x) -> p c x", c=2)

                    def fbu(e):
                        for cc in range(2):
                            for hf in range(2):
                                ins = e.matmul(self.PA[:, cc * 1024 + hf * 512:cc * 1024 + hf * 512 + 512], uv[:, cc, :], bv[:, cc, hf * 512:(hf + 1) * 512], start=True, stop=True)
                        return ins
                    T.op('pe', fbu, r=[f"uT{p}", f"bblk{z}"], w=[("ps", 0), ("ps", 1), ("ps", 2), ("ps", 3)])
                    PAr, PAi = cview(self.PA[:, :], 0), cview(self.PA[:, :], 1)
                    pa = [("ps", 0), ("ps", 1), ("ps", 2), ("ps", 3)]
                    T.op('dve', lambda e: e.tensor_tensor(out=v3(t1), in0=PAr, in1=nr, op=ALU.mult), r=pa + [f"t{z}nr"], w=["t1"])
                    T.op('dve', lambda e: e.tensor_tensor(out=v3(t2), in0=PAi, in1=ni, op=ALU.mult), r=pa + [f"t{z}ni"], w=["t2"])
                    T.op('dve', lambda e: e.tensor_tensor(out=cview(Kt, 0), in0=v3(t1), in1=v3(t2), op=ALU.subtract), r=["t1", "t2"], w=["Kt"])
                    T.op('dve', lambda e: e.tensor_tensor(out=v3(t1), in0=PAi, in1=nr, op=ALU.mult), r=pa + [f"t{z}nr"], w=["t1"])
                    T.op('dve', lambda e: e.tensor_tensor(out=v3(t2), in0=PAr, in1=ni, op=ALU.mult), r=pa + [f"t{z}ni"], w=["t2"])
                    T.op('dve', lambda e: e.tensor_tensor(out=cview(Kt, 1), in0=v3(t1), in1=v3(t2), op=ALU.add), r=["t1", "t2"], w=["Kt"])

                    def fx(e):
                        for q in range(4):
                            sl = slice(q * 512, (q + 1) * 512)
                            e.matmul(self.PB[:, sl], tri[:, z * 128:(z + 1) * 128], Kt[:, sl], start=True, stop=False)
                            e.matmul(self.PB[:, sl], self.onesrow[:], xhi[:, sl], start=False, stop=False)
                            ins = e.matmul(self.PB[:, sl], self.onesrow[:], xlo[:, sl], start=False, stop=True)
                        return ins
                    pb = [("ps", 4), ("ps", 5), ("ps", 6), ("ps", 7)]
                    T.op('pe', fx, r=["Kt", "tri", "onesrow", "xhi", "xlo"], w=pb)

                    def fsum(e):
                        for q in range(4):
                            sl = slice(q * 512, (q + 1) * 512)
                            e.matmul(self.PA[0:1, sl], self.ones[:, 0:1], Kt[:, sl], start=True, stop=False)
                            e.matmul(self.PA[0:1, sl], self.onesrow[:, 0:1], xhi[:, sl], start=False, stop=False)
                            ins = e.matmul(self.PA[0:1, sl], self.onesrow[:, 0:1], xlo[:, sl], start=False, stop=True)
                        return ins
                    T.op('pe', fsum, r=["Kt", "ones", "onesrow", "xhi", "xlo"], w=pa)
                    T.op('act', lambda e: e.activation(out=srow[0:1, :], in_=self.PA[0:1, :], func=AF.Copy), r=pa, w=["srow"])
                    sr, si_ = cview(srow, 0)[0:1], cview(srow, 1)[0:1]
                    L3r, L3i = Lr.rearrange("p (g q) -> p g q", g=G), Li.rearrange("p (g q) -> p g q", g=G)
                    w1_, w2_ = v3(t1)[0:1], v3(t2)[0:1]
                    T.op('dve', lambda e: e.tensor_tensor(out=w1_, in0=sr, in1=L3r, op=ALU.mult), r=["srow", f"l128{z}r"], w=["t1"])
                    T.op('dve', lambda e: e.tensor_tensor(out=w2_, in0=si_, in1=L3i, op=ALU.mult), r=["srow", f"l128{z}i"], w=["t2"])
                    T.op('dve', lambda e: e.tensor_tensor(out=cview(xin, 0)[0:1], in0=w1_, in1=w2_, op=ALU.subtract), r=["t1", "t2"], w=["xin"])
                    T.op('dve', lambda e: e.tensor_tensor(out=w1_, in0=sr, in1=L3i, op=ALU.mult), r=["srow", f"l128{z}i"], w=["t1"])
                    T.op('dve', lambda e: e.tensor_tensor(out=w2_, in0=si_, in1=L3r, op=ALU.mult), r=["srow", f"l128{z}r"], w=["t2"])
                    T.op('dve', lambda e: e.tensor_tensor(out=cview(xin, 1)[0:1], in0=w1_, in1=w2_, op=ALU.add), r=["t1", "t2"], w=["xin"])
                    T.op('dve', lambda e: e.tensor_copy(xhi[0:1, :], xin[0:1, :]), r=["xin"], w=["xhi"])
                    T.op('dve', lambda e: e.tensor_tensor(out=srow[0:1, :], in0=xin[0:1, :], in1=xhi[0:1, :], op=ALU.subtract), r=["xin", "xhi"], w=["srow"])
                    T.op('dve', lambda e: e.tensor_copy(xlo[0:1, :], srow[0:1, :]), r=["srow"], w=["xlo"])
                    PBr, PBi = cview(self.PB[:, :], 0), cview(self.PB[:, :], 1)
                    T.op('dve', lambda e: e.tensor_tensor(out=v3(t1), in0=PBr, in1=pr_, op=ALU.mult), r=pb + [f"t{z}pr"], w=["t1"])
                    T.op('dve', lambda e: e.tensor_tensor(out=v3(t2), in0=PBi, in1=pi_, op=ALU.mult), r=pb + [f"t{z}pi"], w=["t2"])
                    T.op('dve', lambda e: e.tensor_tensor(out=cview(X, 0), in0=v3(t1), in1=v3(t2), op=ALU.subtract), r=["t1", "t2"], w=["X"])
                    T.op('dve', lambda e: e.tensor_tensor(out=v3(t1), in0=PBi, in1=pr_, op=ALU.mult), r=pb + [f"t{z}pr"], w=["t1"])
                    T.op('dve', lambda e: e.tensor_tensor(out=v3(t2), in0=PBr, in1=pi_, op=ALU.mult), r=pb + [f"t{z}pi"], w=["t2"])
                    T.op('dve', lambda e: e.scalar_tensor_tensor(out=cview(X, 1), in0=v3(t1), scalar=-1.0, in1=v3(t2), op0=ALU.mult, op1=ALU.subtract), r=["t1", "t2"], w=["X"])

                    def ftr(e):
                        for g in range(G):
                            ins = e.transpose(self.PAb[:, 2048 + g * 128:2048 + (g + 1) * 128], X[:, g * 128:(g + 1) * 128], self.ident[:])
                        return ins
                    T.op('pe', ftr, r=["X", "ident"], w=[("ps", 2), ("ps", 3)])
                    T.op('act', lambda e: e.activation(out=XT, in_=self.PAb[:, 2048:4096], func=AF.Copy), r=[("ps", 2), ("ps", 3)], w=["XT"])
                    cv = cblk[z].rearrange("p (g h) -> p g h", g=G)

                    def fy(e):
                        for g in range(G):
                            ins = e.matmul(self.PB[:, g * 16:(g + 1) * 16], XT[:, g * 128:(g + 1) * 128], cv[:, g, :], start=True, stop=True)
                        return ins
                    T.op('pe', fy, r=["XT", f"cblk{z}"], w=[("ps", 4)])
                    if z == 0:
                        T.op('act', lambda e: e.activation(out=ysb, in_=self.PB[:, 0:256], func=AF.Copy), r=[("ps", 4)], w=["g1"])
                        self.store(s['yf'][n * 128:(n + 1) * 128, :], ysb, r=["g1"])
                    else:
                        T.op('dve', lambda e: e.tensor_tensor(out=g1, in0=self.PB[:, 0:256], in1=yfl[p], op=ALU.add), r=[("ps", 4), f"yfl{p}"], w=["g1"])
                        T.op('dve', lambda e: e.tensor_tensor(out=g2, in0=utl[p], in1=drow, op=ALU.mult), r=[f"utl{p}", "drow"], w=["g2"])
                        T.op('dve', lambda e: e.tensor_tensor(out=g1, in0=g1, in1=g2, op=ALU.add), r=["g1", "g2"], w=["g1"])
                        T.op('dve', lambda e: e.tensor_tensor(out=g2, in0=g1, in1=g1, op=ALU.mult), r=["g1"], w=["g2"])
                        T.op('dve', lambda e: e.tensor_scalar(out=g2, in0=g2, scalar1=0.044715, scalar2=1.0, op0=ALU.mult, op1=ALU.add), r=["g2"], w=["g2"])
                        T.op('dve', lambda e: e.tensor_tensor(out=g2, in0=g2, in1=g1, op=ALU.mult), r=["g2", "g1"], w=["g2"])
                        T.op('act', lambda e: e.activation(out=g2, in_=g2, func=AF.Sigmoid, scale=2.0 * math.sqrt(2.0 / math.pi)), r=["g2"], w=["g2"])
                        T.op('dve', lambda e: e.tensor_tensor(out=g1, in0=g1, in1=g2, op=ALU.mult), r=["g1", "g2"], w=["g1"])
                        T.op('act', lambda e: e.activation(out=sbf, in_=g1, func=AF.Copy), r=["g1"], w=["sbf"])

                        def ft2(e):
                            for c in range(2):
                                ins = e.transpose(self.PAb[:, 2048 + c * 128:2048 + (c + 1) * 128], sbf[:, c * 128:(c + 1) * 128], self.ident[:])
                            return ins
                        T.op('pe', ft2, r=["sbf", "ident"], w=[("ps", 2)])
                        T.op('act', lambda e: e.activation(out=sT, in_=self.PAb[:, 2048:2048 + 256], func=AF.Copy), r=[("ps", 2)], w=["sT"])

                        def fg(e):
                            for kc in range(2):
                                ins = e.matmul(self.PB[:, 512:768], sT[:, kc * 128:(kc + 1) * 128], gw[:, kc * 256:(kc + 1) * 256], start=(kc == 0), stop=(kc == 1))
                            return ins
                        T.op('pe', fg, r=["sT", "gw"], w=[("ps", 5)])
                        T.op('dve', lambda e: e.tensor_tensor(out=g2, in0=self.PB[:, 512:768], in1=gbrow, op=ALU.add), r=[("ps", 5), "gbrow"], w=["g2"])
                        T.op('act', lambda e: e.activation(out=g2, in_=g2, func=AF.Sigmoid), r=["g2"], w=["g2"])
                        T.op('dve', lambda e: e.tensor_tensor(out=g1, in0=g1, in1=g2, op=ALU.mult), r=["g1", "g2"], w=["g1"])
                        T.op('act', lambda e: e.activation(out=g2, in_=g1, func=AF.Square), r=["g1"], w=["g2"])
                        T.op('dve', lambda e: e.tensor_reduce(out=s8[:, 0:1], in_=g2, axis=AX.X, op=ALU.add), r=["g2"], w=["s8"])
                        T.op('act', lambda e: e.activation(out=s8[:, 0:1], in_=s8[:, 0:1], func=AF.Sqrt, scale=1.0 / 256, bias=EPS), r=["s8"], w=["s8"])
                        T.op('dve', lambda e: e.reciprocal(s8[:, 1:2], s8[:, 0:1]), r=["s8"], w=["s8"])
                        T.op('dve', lambda e: e.scalar_tensor_tensor(out=sn, in0=g1, scalar=s8[:, 1:2], in1=sgrow, op0=ALU.mult, op1=ALU.mult), r=["g1", "s8", "sgrow"], w=["sn"])

                        def ft3(e):
                            for c in range(2):
                                ins = e.transpose(self.PAb[:, 2048 + 512 + c * 128:2048 + 512 + (c + 1) * 128], sn[:, c * 128:(c + 1) * 128], self.ident[:])
                            return ins
                        T.op('pe', ft3, r=["sn", "ident"], w=[("ps", 2)])
                        T.op('act', lambda e: e.activation(out=mo, in_=self.PAb[:, 2048 + 512:2048 + 768], func=AF.Copy), r=[("ps", 2)], w=["mo"])
                        self.store(s['mix'][3:5, :, n * 128:(n + 1) * 128].rearrange("c p t -> p c t"), mo.rearrange("p (c t) -> p c t", c=2), r=["mo"])
            T.barrier()

    def layer_norm(self, zt, gcol, bcol, bufs):
        T = self.T
        zb, zq, msb, vsb, tt = bufs
        z3 = zt.rearrange("p (k t) -> p k t", k=8)
        zk = [("z", c) for c in range(8)]
        T.op('act', lambda e: e.activation(out=zb, in_=zt, func=AF.Copy), r=zk, w=["zb"])
        T.op('act', lambda e: e.activation(out=zq, in_=zt, func=AF.Square), r=zk, w=["zq"])
        zb3 = zb.rearrange("p (k t) -> p k t", k=8); zq3 = zq.rearrange("p (k t) -> p k t", k=8)

        def f(e):
            for kc in range(8):
                e.matmul(self.bank(6), self.ones[:], zb3[:, kc, :], start=(kc == 0), stop=(kc == 7))
            for kc in range(8):
                ins = e.matmul(self.bank(7), self.ones[:], zq3[:, kc, :], start=(kc == 0), stop=(kc == 7))
            return ins
        T.op('pe', f, r=["zb", "zq", "ones"], w=[("ps", 6), ("ps", 7)])
        T.op('act', lambda e: e.activation(out=msb, in_=self.bank(6), func=AF.Copy, scale=1.0 / D), r=[("ps", 6)], w=["msb"])
        T.op('dve', lambda e: e.tensor_tensor(out=vsb, in0=msb, in1=msb, op=ALU.mult), r=["msb"], w=["vsb"])
        T.op('dve', lambda e: e.scalar_tensor_tensor(out=vsb, in0=self.bank(7), scalar=1.0 / D, in1=vsb, op0=ALU.mult, op1=ALU.subtract), r=[("ps", 7), "vsb"], w=["vsb"])
        T.op('act', lambda e: e.activation(out=vsb, in_=vsb, func=AF.Sqrt, bias=EPS), r=["vsb"], w=["vsb"])
        T.op('dve', lambda e: e.reciprocal(vsb, vsb), r=["vsb"], w=["vsb"])
        for c in range(8):
            q = c % 2
            T.op('dve', lambda e, c=c, q=q: e.tensor_tensor(out=tt[q], in0=z3[:, c, :], in1=msb, op=ALU.subtract), r=[("z", c), "msb"], w=[f"tt{q}"])
            T.op('dve', lambda e, c=c, q=q: e.tensor_tensor(out=tt[q], in0=tt[q], in1=vsb, op=ALU.mult), r=[f"tt{q}", "vsb"], w=[f"tt{q}"])
            T.op('act', lambda e, c=c, q=q: e.activation(out=z3[:, c, :], in_=tt[q], func=AF.Identity, scale=gcol[:, c:c + 1], bias=bcol[:, c:c + 1]),
                 r=[f"tt{q}", "lnp"], w=[("z", c)])

    def load_cols(self, dst, src1d, ncol, key):
        with self.nc.allow_non_contiguous_dma(reason="param cols"):
            self.load(dst, src1d.rearrange("(c p) -> p c", p=128), w=[key])

    def stage5a(self):
        T, I, l = self.T, self.I, self.l
        self.reset_big()
        wb = self.wbuf[:, 0:8 * D].rearrange("p (k c) -> p k c", k=8)
        for kc in range(8):
            self.load_w(self.wbuf[:, kc * D:(kc + 1) * D], I['wo'][l, kc * 128:(kc + 1) * 128, :], D, "wbuf")
        self.woff = 4096
        gcol = self.carve(8, region='w'); bcol = self.carve(8, region='w')
        self.load_cols(gcol, I['ln1_g'][l], 8, "lnp")
        self.load_cols(bcol, I['ln1_b'][l], 8, "lnp")
        xf = [self.carve(4096) for _ in range(2)]
        mx = [self.carve(2048, BF16) for _ in range(2)]
        zt = self.carve(4096)
        bufs = (self.carve(2048, BF16, region='w'), self.carve(2048, BF16, region='w'), self.carve(512, region='w'), self.carve(512, region='w'), [self.carve(512, region='w'), self.carve(512, region='w')])
        jobs = [(s, t0) for s in self.S for t0 in range(0, s['L'], 512)]
        src = lambda s: (s['xin'] if l == 0 else s['xa'])

        def ld(j):
            s, t0 = jobs[j]; p = j % 2
            self.load(xf[p].rearrange("p (k t) -> p k t", k=8), src(s)[:, :, t0:t0 + 512].rearrange("c p t -> p c t"), w=[f"xf{p}"])
            self.load(mx[p].rearrange("p (k t) -> p k t", k=8), s['mix'][:, :, t0:t0 + 512].rearrange("c p t -> p c t"), w=[f"mx{p}"])
        ld(0)
        for j, (s, t0) in enumerate(jobs):
            p = j % 2
            if j + 1 < len(jobs):
                ld(j + 1)
            mv = mx[p].rearrange("p (k t) -> p k t", k=8)
            xv = xf[p].rearrange("p (k t) -> p k t", k=8)
            z3 = zt.rearrange("p (k t) -> p k t", k=8)
            for oc in range(8):
                b = oc % 4

                def f(e, oc=oc, b=b):
                    for kc in range(8):
                        ins = e.matmul(self.bank(b), wb[:, kc, oc * 128:(oc + 1) * 128], mv[:, kc, :], start=(kc == 0), stop=(kc == 7))
                    return ins
                T.op('pe', f, r=["wbuf", f"mx{p}"], w=[("ps", b)])
                T.op('dve', lambda e, oc=oc, b=b: e.scalar_tensor_tensor(out=z3[:, oc, :], in0=xv[:, oc, :], scalar=ALPHA, in1=self.bank(b), op0=ALU.mult, op1=ALU.add),
                     r=[f"xf{p}", ("ps", b)], w=[("z", oc)])
            self.layer_norm(zt, gcol, bcol, bufs)
            self.store(s['x1'][:, :, 1 + t0:1 + t0 + 512].rearrange("c p t -> p c t"), zt.rearrange("p (k t) -> p k t", k=8), r=[("z", c) for c in range(8)])
        T.barrier()

    def stage5b(self):
        T, I, l = self.T, self.I, self.l
        self.reset_big()
        NU = 2 * DFF
        wb = self.wbuf[:, 0:8 * NU].rearrange("p (k c) -> p k c", k=8)
        for kc in range(8):
            self.load_w(self.wbuf[:, kc * NU:(kc + 1) * NU], I['wu'][l, kc * 128:(kc + 1) * 128, :], NU, "wbuf")
        cw = self.carve(3 * NH); cb = self.carve(NH)
        with self.nc.allow_non_contiguous_dma(reason="param cols"):
            self.load(cw.rearrange("p (k m) -> p k m", k=3), I['ffn_conv_w'][l].rearrange("k (m p) -> p k m", p=128), w=["cw"])
        self.load_cols(cb, I['ffn_conv_b'][l], NH, "cb")
        cwv = cw.rearrange("p (k m) -> p k m", k=3)
        xf = [self.carve(8 * 514) for _ in range(2)]
        xb = self.carve(4 * 514, BF16)
        asb = [self.carve(516) for _ in range(2)]
        csb = [self.carve(512) for _ in range(2)]
        hid = self.carve(NH * 256, BF16)
        xbv = xb.rearrange("p (k t) -> p k t", k=8)
        hv = hid.rearrange("p (m t) -> p m t", m=NH)
        jobs = [(s, t0) for s in self.S for t0 in range(0, s['L'], 512)]

        def ld(j):
            s, t0 = jobs[j]; p = j % 2
            self.load(xf[p].rearrange("p (k t) -> p k t", k=8), s['x1'][:, :, t0:t0 + 514].rearrange("c p t -> p c t"), w=[f"xf{p}"])
        ld(0)
        for j, (s, t0) in enumerate(jobs):
            p = j % 2
            if j + 1 < len(jobs):
                ld(j + 1)
            T.op('act', lambda e, p=p: e.activation(out=xb, in_=xf[p], func=AF.Copy), r=[f"xf{p}"], w=["xb"])
            for m in range(NH):
                q = m % 2
                bA, bB, bC = 3 * q, 3 * q + 1, 3 * q + 2

                def f(e, m=m, bA=bA, bB=bB, bC=bC):
                    for kc in range(8):
                        e.matmul(self.bank(bA), wb[:, kc, m * 128:(m + 1) * 128], xbv[:, kc, 1:513], start=(kc == 0), stop=(kc == 7))
                    for kc in range(8):
                        e.matmul(self.bank(bB), wb[:, kc, DFF + m * 128:DFF + (m + 1) * 128], xbv[:, kc, 0:512], start=(kc == 0), stop=(kc == 7))
                    for kc in range(8):
                        ins = e.matmul(self.bank(bC)[:, 0:2], wb[:, kc, DFF + m * 128:DFF + (m + 1) * 128], xbv[:, kc, 512:514], start=(kc == 0), stop=(kc == 7))
                    return ins
                T.op('pe', f, r=["wbuf", "xb"], w=[("ps", bA), ("ps", bB), ("ps", bC)])
                a = asb[q]; c = csb[q]
                T.op('act', lambda e, a=a, bB=bB: e.activation(out=a[:, 0:512], in_=self.bank(bB), func=AF.Copy), r=[("ps", bB)], w=[f"asb{q}"])
                T.op('act', lambda e, a=a, bC=bC: e.activation(out=a[:, 512:514], in_=self.bank(bC)[:, 0:2], func=AF.Copy), r=[("ps", bC)], w=[f"asb{q}"])
                T.op('act', lambda e, a=a, c=c, m=m: e.activation(out=c, in_=a[:, 1:513], func=AF.Identity, scale=cwv[:, 1, m:m + 1], bias=cb[:, m:m + 1]),
                     r=[f"asb{q}", "cw", "cb"], w=[f"csb{q}"])
                T.op('dve', lambda e, a=a, c=c, m=m: e.scalar_tensor_tensor(out=c, in0=a[:, 0:512], scalar=cwv[:, 0, m:m + 1], in1=c, op0=ALU.mult, op1=ALU.add),
                     r=[f"asb{q}", f"csb{q}", "cw"], w=[f"csb{q}"])
                T.op('dve', lambda e, a=a, c=c, m=m: e.scalar_tensor_tensor(out=c, in0=a[:, 2:514], scalar=cwv[:, 2, m:m + 1], in1=c, op0=ALU.mult, op1=ALU.add),
                     r=[f"asb{q}", f"csb{q}", "cw"], w=[f"csb{q}"])
                T.op('act', lambda e, c=c: e.activation(out=c, in_=c, func=AF.Silu), r=[f"csb{q}"], w=[f"csb{q}"])
                T.op('dve', lambda e, c=c, m=m, bA=bA: e.tensor_tensor(out=hv[:, m, :], in0=self.bank(bA), in1=c, op=ALU.mult), r=[("ps", bA), f"csb{q}"], w=["hid"])
            self.store(s['hid'][:, :, t0:t0 + 512].rearrange("c p t -> p c t"), hv, r=["hid"])
        T.barrier()

    def stage5c(self):
        T, I, l = self.T, self.I, self.l
        self.reset_big()
        wb = self.wbuf[:, 0:NH * D].rearrange("p (k c) -> p k c", k=NH)
        for kc in range(NH):
            self.load_w(self.wbuf[:, kc * D:(kc + 1) * D], I['wd'][l, kc * 128:(kc + 1) * 128, :], D, "wbuf")
        self.woff = 11264
        gcol = self.carve(8, region='w'); bcol = self.carve(8, region='w')
        self.load_cols(gcol, I['ln2_g'][l], 8, "lnp")
        self.load_cols(bcol, I['ln2_b'][l], 8, "lnp")
        xf = [self.carve(4096) for _ in range(2)]
        hd = [self.carve(NH * 256, BF16)] * 2
        zt = self.carve(4096)
        bufs = (self.carve(2048, BF16, region='w'), self.carve(2048, BF16, region='w'), self.carve(512, region='w'), self.carve(512, region='w'), [self.carve(512, region='w'), self.carve(512, region='w')])
        jobs = [(s, t0) for s in self.S for t0 in range(0, s['L'], 512)]
        last = (l == self.depth - 1)

        def ld(j):
            s, t0 = jobs[j]; p = j % 2
            self.load(xf[p].rearrange("p (k t) -> p k t", k=8), s['x1'][:, :, 1 + t0:1 + t0 + 512].rearrange("c p t -> p c t"), w=[f"xf{p}"])
        ld(0)
        for j, (s, t0) in enumerate(jobs):
            p = j % 2
            if j + 1 < len(jobs):
                ld(j + 1)
            self.load(hd[p].rearrange("p (k t) -> p k t", k=NH), s['hid'][:, :, t0:t0 + 512].rearrange("c p t -> p c t"), w=["hd"])
            hv = hd[p].rearrange("p (k t) -> p k t", k=NH)
            xv = xf[p].rearrange("p (k t) -> p k t", k=8)
            z3 = zt.rearrange("p (k t) -> p k t", k=8)
            for oc in range(8):
                b = oc % 4

                def f(e, oc=oc, b=b):
                    for kc in range(NH):
                        ins = e.matmul(self.bank(b), wb[:, kc, oc * 128:(oc + 1) * 128], hv[:, kc, :], start=(kc == 0), stop=(kc == NH - 1))
                    return ins
                T.op('pe', f, r=["wbuf", "hd"], w=[("ps", b)])
                T.op('dve', lambda e, oc=oc, b=b: e.scalar_tensor_tensor(out=z3[:, oc, :], in0=xv[:, oc, :], scalar=ALPHA, in1=self.bank(b), op0=ALU.mult, op1=ALU.add),
                     r=[f"xf{p}", ("ps", b)], w=[("z", oc)])
            self.layer_norm(zt, gcol, bcol, bufs)
            dst = s['yout'] if last else s['xa']
            self.store(dst[:, :, t0:t0 + 512].rearrange("c p t -> p c t"), zt.rearrange("p (k t) -> p k t", k=8), r=[("z", c) for c in range(8)])
        T.barrier()


_CACHE = {}


def _prep_shared(inputs, depth, lmax):
    perm = _perm_w_in()
    sh = {}
    sh['w1'] = np.ascontiguousarray(inputs['w_in'][:depth][:, :, perm])
    sh['wo'] = np.ascontiguousarray(inputs['w_out'][:depth])
    sh['wu'] = np.ascontiguousarray(inputs['ffn_w_up'][:depth])
    sh['wd'] = np.ascontiguousarray(inputs['ffn_w_down'][:depth])
    ct = _const_tables(lmax)
    sh.update(ct)
    for nm in ["attn_sink", "attn_out_g", "ssm_lambda_re", "ssm_lambda_im", "ssm_log_dt", "ssm_b_re", "ssm_b_im",
               "ssm_c_re", "ssm_c_im", "ssm_d", "ssm_glu_w", "ssm_glu_b", "ssm_out_g", "ln1_g", "ln1_b",
               "ffn_conv_w", "ffn_conv_b", "ln2_g", "ln2_b"]:
        sh[nm] = np.ascontiguousarray(inputs[nm][:depth]).astype(np.float32)
    return sh


def _to_fm(x):
    L = x.shape[0]
    return np.ascontiguousarray(x.T.reshape(8, 128, L))


def _from_fm(y):
    L = y.shape[2]
    return np.ascontiguousarray(y.reshape(1024, L).T)


def run_seqs(inputs, per_core_seqs, depth=DEPTH, lmax=LMAX):
    lens = tuple(x.shape[0] for x in per_core_seqs[0])
    key = (lens, depth, lmax)
    if key not in _CACHE:
        _CACHE[key] = Builder(list(lens), depth, lmax).build()
    nc = _CACHE[key]
    sh = _prep_shared(inputs, depth, lmax)
    in_maps = []
    for seqs in per_core_seqs:
        m = dict(sh)
        for i, x in enumerate(seqs):
            m[f"x{i}"] = _to_fm(np.asarray(x, np.float32))
        in_maps.append(m)
    n = len(per_core_seqs)
    res = run_bass_kernel_spmd(nc, in_maps, core_ids=list(range(n)))
    outs = []
    for c in range(n):
        outs.append([_from_fm(np.asarray(res.results[c][f"y{i}"])) for i in range(len(lens))])
    return outs


def kernel(**inputs):
    xp = np.asarray(inputs['x_prompt'], np.float32)
    xs = np.asarray(inputs['x_sample'], np.float32)
    per_core = []
    for c in range(8):
        per_core.append([xp[2 * c], xp[2 * c + 1], xs[c // 4]])
    outs = run_seqs(inputs, per_core)
    yp = np.stack([outs[c][k] for c in range(8) for k in range(2)], 0)
    ys = np.zeros_like(xs)
    for b in range(2):
        for q in range(4):
            ys[b, q * 4096:(q + 1) * 4096] = outs[4 * b + q][2][q * 4096:(q + 1) * 4096]
    return (yp.astype(np.float32), ys.astype(np.float32))
```
